# Optimizing a Trainium2 kernel written in Bass

```python
import math
import jax, jax.numpy as jnp
from jax import lax
import numpy as np

D_MODEL = 4096
BATCH = 1
SEQ = 8192
DEPTH = 4

HEAD_DIM = 128
ROT_DIM = HEAD_DIM // 4
ROPE_THETA = 500000.0
NORM_EPS = 1e-5
MIX_WIDTH = 3 * D_MODEL // 8
DIFF_HEADS = MIX_WIDTH // (2 * HEAD_DIM)
DIFF_QK = DIFF_HEADS * 2 * HEAD_DIM
DIFF_V = DIFF_HEADS * 2 * HEAD_DIM
DIFF_Q_BLOCK = 128
DIL_PATTERNS = ((128, 1), (512, 4), (2048, 16))
DIL_GROUPS = 3
DIL_HEADS_PER_GROUP = 4
DIL_HEADS = DIL_GROUPS * DIL_HEADS_PER_GROUP
DIL_WIDTH = DIL_HEADS * HEAD_DIM
DIL_OUT = DIL_HEADS_PER_GROUP * HEAD_DIM
DIL_BLOCK = 64
CONV_WIDTH = MIX_WIDTH
CONV_K = 3
POOL_WINDOWS = (2, 4, 8, 16)
POOL_WIDTH = MIX_WIDTH
POOL_GROUP = POOL_WIDTH // 4
N_BRANCHES = 4
GATE_RANK = D_MODEL // 8
D_FF = D_MODEL

kernel_name = "hybrid_parallel_gated_mixer_encoder"


def rmsnorm(x, g):
    x32 = x.astype(jnp.float32)
    y = x32 * lax.rsqrt(jnp.mean(x32 * x32, axis=-1, keepdims=True) + NORM_EPS)
    return (y * g.astype(jnp.float32)).astype(x.dtype)


def swiglu(h, wg, wu, wd):
    return (jax.nn.silu(h @ wg) * (h @ wu)) @ wd


def rope_tables(seq):
    inv = ROPE_THETA ** (-jnp.arange(0, ROT_DIM, 2, dtype=jnp.float32) / ROT_DIM)
    ang = jnp.arange(seq, dtype=jnp.float32)[:, None] * inv[None, :]
    return jnp.cos(ang), jnp.sin(ang)


def apply_partial_rope(t, cos, sin):
    shp = (1, cos.shape[0]) + (1,) * (t.ndim - 3) + (cos.shape[1],)
    c = cos.reshape(shp).astype(t.dtype)
    s = sin.reshape(shp).astype(t.dtype)
    half = ROT_DIM // 2
    t1, t2, rest = t[..., :half], t[..., half:ROT_DIM], t[..., ROT_DIM:]
    return jnp.concatenate([t1 * c - t2 * s, t2 * c + t1 * s, rest], axis=-1)


def diff_attention(q, k, v, lam, subln_g, lam_init):
    b, s, h = q.shape[:3]
    nq = s // DIFF_Q_BLOCK
    scale = HEAD_DIM ** -0.5
    qb = q.reshape(b, nq, DIFF_Q_BLOCK, h, 2, HEAD_DIM).transpose(1, 0, 2, 3, 4, 5)

    def block(qblk):
        sc = jnp.einsum('bqhcd,bkhcd->bchqk', qblk, k).astype(jnp.float32) * scale
        p = jax.nn.softmax(sc, axis=-1)
        a = (p[:, 0] - lam * p[:, 1]).astype(v.dtype)
        return jnp.einsum('bhqk,bkhe->bqhe', a, v)

    o = lax.map(block, qb)
    o = o.transpose(1, 0, 2, 3, 4).reshape(b, s, h, 2 * HEAD_DIM)
    o = rmsnorm(o, subln_g) * (1.0 - lam_init)
    return o.reshape(b, s, h * 2 * HEAD_DIM)


def banded_attention(q, k, v, radius):
    n, l, hd = q.shape
    blk = DIL_BLOCK
    nb = -(-l // blk)
    pad_q = nb * blk - l
    qp = jnp.pad(q, ((0, 0), (0, pad_q), (0, 0))).reshape(n, nb, blk, hd)

    def windows(t):
        tp = jnp.pad(t, ((0, 0), (blk, pad_q + blk), (0, 0))).reshape(n, nb + 2, blk, hd)
        return jnp.concatenate([tp[:, :-2], tp[:, 1:-1], tp[:, 2:]], axis=2)

    kw, vw = windows(k), windows(v)
    sc = jnp.einsum('nbqd,nbkd->nbqk', qp, kw).astype(jnp.float32) * (hd ** -0.5)
    qi = jnp.arange(nb)[:, None, None] * blk + jnp.arange(blk)[None, :, None]
    kj = (jnp.arange(nb)[:, None, None] - 1) * blk + jnp.arange(3 * blk)[None, None, :]
    valid = (jnp.abs(qi - kj) <= radius) & (kj >= 0) & (kj < l)
    sc = jnp.where(valid, sc, -1e30)
    m = jnp.max(sc, axis=-1, keepdims=True)
    p = jnp.exp(sc - m)
    den = jnp.sum(p, axis=-1, keepdims=True)
    o = jnp.einsum('nbqk,nbkd->nbqd', (p / den).astype(v.dtype), vw)
    lse = (m + jnp.log(den))[..., 0]
    return o.reshape(n, nb * blk, hd)[:, :l], lse.reshape(n, nb * blk)[:, :l]


def dilated_attention(q, k, v):
    b, s = q.shape[:2]
    hg = DIL_HEADS_PER_GROUP
    outs, lses = [], []
    for g, (window, dil) in enumerate(DIL_PATTERNS):
        radius = window // (2 * dil)
        lsub = s // dil

        def gather(t):
            t = t[:, :, g * hg:(g + 1) * hg].reshape(b, lsub, dil, hg, HEAD_DIM)
            return t.transpose(0, 2, 3, 1, 4).reshape(b * dil * hg, lsub, HEAD_DIM)

        o, lse = banded_attention(gather(q), gather(k), gather(v), radius)
        outs.append(o.reshape(b, dil, hg, lsub, HEAD_DIM).transpose(0, 3, 1, 2, 4).reshape(b, s, hg, HEAD_DIM))
        lses.append(lse.reshape(b, dil, hg, lsub).transpose(0, 3, 1, 2).reshape(b, s, hg))
    w = jax.nn.softmax(jnp.stack(lses, axis=0), axis=0)
    o = jnp.einsum('gbsh,gbshd->bshd', w.astype(q.dtype), jnp.stack(outs, axis=0))
    return o.reshape(b, s, DIL_OUT)


def short_conv_mixer(gate_b, gate_c, u, w_conv):
    z = gate_c * u
    y = lax.conv_general_dilated(z, w_conv[:, None, :].astype(z.dtype), window_strides=(1,),
                                 padding=((CONV_K // 2, CONV_K // 2),),
                                 dimension_numbers=('NWC', 'WIO', 'NWC'),
                                 feature_group_count=z.shape[-1])
    return gate_b * y


def pool_mixer(u, w_pool, pool_scale):
    b, s, _ = u.shape
    ug = u.reshape(b, s, 4, POOL_GROUP)
    csum = jnp.concatenate([jnp.zeros((b, 1, 4, POOL_GROUP), jnp.float32),
                            jnp.cumsum(ug.astype(jnp.float32), axis=1)], axis=1)
    pos = jnp.arange(s)
    pooled = []
    for gi, win in enumerate(POOL_WINDOWS):
        lo = jnp.clip(pos - win // 2, 0, s)
        hi = jnp.clip(pos - win // 2 + win, 0, s)
        cg = csum[:, :, gi]
        cnt = (hi - lo).astype(jnp.float32)[None, :, None]
        pooled.append((cg[:, hi] - cg[:, lo]) / cnt)
    pooled = jnp.stack(pooled, axis=2).astype(u.dtype) - ug
    y = jnp.einsum('bsgc,gcd->bsgd', pooled, w_pool)
    return y.reshape(b, s, POOL_WIDTH) * pool_scale


def setup_inputs(seed: int = 0) -> dict:
    key = jax.random.key(seed)
    k = jax.random.split(key, 27)
    L, D = DEPTH, D_MODEL

    def dense(kk, shape, fan_in):
        return jax.random.normal(kk, shape, jnp.float32) * fan_in ** -0.5

    def gain(kk, shape):
        return 1.0 + 0.02 * jax.random.normal(kk, shape, jnp.float32)

    in_cols = 2 * DIFF_QK + DIFF_V + 3 * DIL_WIDTH + 3 * CONV_WIDTH + POOL_WIDTH + GATE_RANK
    return {
        'x': jax.random.normal(k[0], (BATCH, SEQ, D), jnp.float32),
        'ln_ffn1': gain(k[1], (L, D)),
        'w1_gate': dense(k[2], (L, D, D_FF), D),
        'w1_up': dense(k[3], (L, D, D_FF), D),
        'w1_down': dense(k[4], (L, D_FF, D), D_FF),
        'ln_mix': gain(k[5], (L, D)),
        'w_in': dense(k[6], (L, D, in_cols), D),
        'lambda_q1': 0.1 * jax.random.normal(k[7], (L, HEAD_DIM), jnp.float32),
        'lambda_k1': 0.1 * jax.random.normal(k[8], (L, HEAD_DIM), jnp.float32),
        'lambda_q2': 0.1 * jax.random.normal(k[9], (L, HEAD_DIM), jnp.float32),
        'lambda_k2': 0.1 * jax.random.normal(k[10], (L, HEAD_DIM), jnp.float32),
        'subln': gain(k[11], (L, 2 * HEAD_DIM)),
        'conv_w': dense(k[12], (L, CONV_K, CONV_WIDTH), CONV_K),
        'pool_w': dense(k[13], (L, 4, POOL_GROUP, POOL_GROUP), POOL_GROUP),
        'pool_scale': 0.5 + 0.05 * jax.random.normal(k[14], (L, POOL_WIDTH), jnp.float32),
        'w_gate_up': dense(k[15], (L, GATE_RANK, N_BRANCHES * D), GATE_RANK),
        'b_gate': 0.02 * jax.random.normal(k[16], (L, N_BRANCHES * D), jnp.float32),
        'w_branch_a': dense(k[17], (L, DIFF_V, D), DIFF_V),
        'w_branch_b': dense(k[18], (L, DIL_OUT, D), DIL_OUT),
        'w_branch_c': dense(k[19], (L, CONV_WIDTH, D), CONV_WIDTH),
        'w_branch_d': dense(k[20], (L, POOL_WIDTH, D), POOL_WIDTH),
        'w_out': dense(k[21], (L, D, D), D),
        'ln_ffn2': gain(k[22], (L, D)),
        'w2_gate': dense(k[23], (L, D, D_FF), D),
        'w2_up': dense(k[24], (L, D, D_FF), D),
        'w2_down': dense(k[25], (L, D_FF, D), D_FF),
        'ln_final': gain(k[26], (D,)),
    }


def reference(x, ln_ffn1, w1_gate, w1_up, w1_down, ln_mix, w_in, lambda_q1, lambda_k1,
              lambda_q2, lambda_k2, subln, conv_w, pool_w, pool_scale, w_gate_up, b_gate,
              w_branch_a, w_branch_b, w_branch_c, w_branch_d, w_out, ln_ffn2, w2_gate,
              w2_up, w2_down, ln_final):
    b, s, d = x.shape
    cos, sin = rope_tables(s)
    sizes = (DIFF_QK, DIFF_QK, DIFF_V, DIL_WIDTH, DIL_WIDTH, DIL_WIDTH,
             CONV_WIDTH, CONV_WIDTH, CONV_WIDTH, POOL_WIDTH, GATE_RANK)
    split_idx = []
    acc = 0
    for sz in sizes[:-1]:
        acc += sz
        split_idx.append(acc)

    for l in range(DEPTH):
        x = x + 0.5 * swiglu(rmsnorm(x, ln_ffn1[l]), w1_gate[l], w1_up[l], w1_down[l])

        h = rmsnorm(x, ln_mix[l])
        proj = h @ w_in[l]
        qa, ka, va, qb, kb, vb, c_b, c_c, c_u, p_u, g_low = jnp.split(proj, split_idx, axis=-1)

        lam_init = 0.8 - 0.6 * math.exp(-0.3 * l)
        lam = (jnp.exp(jnp.sum(lambda_q1[l] * lambda_k1[l]).astype(jnp.float32))
               - jnp.exp(jnp.sum(lambda_q2[l] * lambda_k2[l]).astype(jnp.float32)) + lam_init)
        qa = apply_partial_rope(qa.reshape(b, s, DIFF_HEADS, 2, HEAD_DIM), cos, sin)
        ka = apply_partial_rope(ka.reshape(b, s, DIFF_HEADS, 2, HEAD_DIM), cos, sin)
        ya = diff_attention(qa, ka, va.reshape(b, s, DIFF_HEADS, 2 * HEAD_DIM), lam, subln[l], lam_init)

        qb = apply_partial_rope(qb.reshape(b, s, DIL_HEADS, HEAD_DIM), cos, sin)
        kb = apply_partial_rope(kb.reshape(b, s, DIL_HEADS, HEAD_DIM), cos, sin)
        yb = dilated_attention(qb, kb, vb.reshape(b, s, DIL_HEADS, HEAD_DIM))

        yc = short_conv_mixer(c_b, c_c, c_u, conv_w[l])

        yd = pool_mixer(p_u, pool_w[l], pool_scale[l])

        gates = jax.nn.sigmoid((g_low @ w_gate_up[l] + b_gate[l]).astype(jnp.float32))
        gates = gates.astype(x.dtype).reshape(b, s, N_BRANCHES, d)
        merged = (gates[:, :, 0] * (ya @ w_branch_a[l]) + gates[:, :, 1] * (yb @ w_branch_b[l])
                  + gates[:, :, 2] * (yc @ w_branch_c[l]) + gates[:, :, 3] * (yd @ w_branch_d[l]))
        x = x + merged @ w_out[l]

        x = x + 0.5 * swiglu(rmsnorm(x, ln_ffn2[l]), w2_gate[l], w2_up[l], w2_down[l])

    return rmsnorm(x, ln_final)
```

```python
import math
from contextlib import ExitStack

import numpy as np
import ml_dtypes

import concourse.bass as bass
import concourse.mybir as mybir
from concourse.bass_utils import run_bass_kernel_spmd

F32 = mybir.dt.float32
BF16 = mybir.dt.bfloat16
ALU = mybir.AluOpType
AF = mybir.ActivationFunctionType

D = 4096
KC = D // 128
HD = 128
MIXW = 1536
IN_COLS = 15872
EPS = 1e-5
T = 1024
EPOCH = 24000
NDMA_SEM = 16
NCORES = 8
SL = 1024
LW = 4096
NEG = -30000.0


class Prog:
    ENGS = ("pe", "dve", "act", "pool", "sp")
    EMAP = {"pe": "tensor", "dve": "vector", "act": "scalar", "pool": "gpsimd", "sp": "sync"}

    def __init__(self, nc, stack):
        self.nc = nc
        self.gstack = stack
        self.stack = stack
        self.streams = {e: [] for e in self.ENGS}
        self.tick = {e: 0 for e in self.ENGS}
        self.sems = {}
        self.last_w = {}
        self.readers = {}
        self.waited = {e: {} for e in self.ENGS}
        self.dma_cnt = {e: 0 for e in self.ENGS}
        self.n_ops = 0
        self.uid = 0

    def sem(self, key):
        s = self.sems.get(key)
        if s is None:
            s = self.gstack.enter_context(self.nc.semaphore("s_" + "_".join(str(k) for k in key)))
            self.sems[key] = s
        return s

    def sbuf(self, name, shape, dt):
        self.uid += 1
        return self.stack.enter_context(self.nc.sbuf_tensor(f"{name}_{self.uid}", list(shape), dt))

    def psum(self, name, shape, dt=F32):
        self.uid += 1
        return self.stack.enter_context(self.nc.psum_tensor(f"{name}_{self.uid}", list(shape), dt))

    def _deps(self, reads, writes):
        evs = []
        lw = self.last_w
        for k in reads:
            w = lw.get(k)
            if w is not None:
                evs.append(w)
        for k in writes:
            w = lw.get(k)
            if w is not None:
                evs.append(w)
            r = self.readers.get(k)
            if r:
                evs.extend(r.items())
        return evs

    def _emit_waits(self, eng, evs):
        wd = self.waited[eng]
        st = self.streams[eng]
        for (skey, val) in evs:
            if eng == "pe" and skey[0] == "t" and skey[1] == "pe":
                continue
            if wd.get(skey, 0) >= val:
                continue
            wd[skey] = val
            st.append(("wait", skey, val))

    def _record(self, ev, reads, writes):
        for k in reads:
            d = self.readers.get(k)
            if d is None:
                d = self.readers[k] = {}
            if d.get(ev[0], 0) < ev[1]:
                d[ev[0]] = ev[1]
        for k in writes:
            self.last_w[k] = ev
            self.readers[k] = {}

    def op(self, eng, fn, reads=(), writes=(), sig=True):
        self._emit_waits(eng, self._deps(reads, writes))
        t = self.tick[eng] + 1
        ep = (t - 1) // EPOCH
        skey = ("t", eng, ep)
        ev = (skey, t - ep * EPOCH)
        if sig:
            self.tick[eng] = t
            self.streams[eng].append(("op", fn, skey, 1))
        else:
            self.streams[eng].append(("op", fn, None, 0))
        self._record(ev, reads, writes)
        self.n_ops += 1
        return ev

    def dma(self, q, out, in_, reads=(), writes=(), slow=False):
        evs = self._deps(reads, writes)
        i = self.dma_cnt[q]
        self.dma_cnt[q] += 1
        slot = i % NDMA_SEM
        rnd = i // NDMA_SEM
        skey = ("d", q, slot)
        if rnd > 0:
            evs.append((skey, 16 * rnd))
        self._emit_waits(q, evs)
        self.streams[q].append(("dma", (out, in_, slow), skey, 16))
        ev = (skey, 16 * (rnd + 1))
        self._record(ev, reads, writes)
        self.n_ops += 1
        return ev

    def cc(self, in_ap, out_ap, reads=(), writes=()):
        self._emit_waits("pool", self._deps(reads, writes))
        self.ncc = getattr(self, "ncc", 0) + 1
        skey = ("cc",)
        groups = [list(range(NCORES))]
        self.streams["pool"].append(("op", lambda e: e.collective_compute("AllReduce", ALU.add, replica_groups=groups,
                                                                          ins=[in_ap], outs=[out_ap]), skey, 1))
        ev = (skey, self.ncc)
        self._record(ev, reads, writes)
        return ev

    def barrier(self):
        evs = []
        for e in self.ENGS:
            t = self.tick[e]
            if t > 0:
                ep = (t - 1) // EPOCH
                evs.append((("t", e, ep), t - ep * EPOCH))
            n = self.dma_cnt[e]
            for slot in range(min(n, NDMA_SEM)):
                cnt = (n - 1 - slot) // NDMA_SEM + 1
                evs.append((("d", e, slot), 16 * cnt))
        if getattr(self, "ncc", 0):
            evs.append((("cc",), self.ncc))
        for e in self.ENGS:
            self._emit_waits(e, evs)
        self.last_w = {}
        self.readers = {}

    def flush(self):
        nc = self.nc
        for e in self.ENGS:
            for it in self.streams[e]:
                if it[0] == "wait":
                    self.sem(it[1])
                elif it[2] is not None:
                    self.sem(it[2])
        sems = self.sems
        with nc.Block() as block:
            for e in self.ENGS:
                stream = self.streams[e]
                if not stream:
                    continue

                def body(engine, stream=stream):
                    for it in stream:
                        k = it[0]
                        if k == "wait":
                            engine.wait_ge(sems[it[1]], it[2])
                        elif k == "op":
                            ins = it[1](engine)
                            if it[2] is not None:
                                ins.then_inc(sems[it[2]], it[3])
                        else:
                            out, in_, slow = it[1]
                            if slow:
                                engine.dma_start(out=out, in_=in_, allow_slow_non_contiguous=True).then_inc(sems[it[2]], it[3])
                            else:
                                engine.dma_start(out=out, in_=in_).then_inc(sems[it[2]], it[3])

                getattr(block, self.EMAP[e])(body)
        self.streams = {e: [] for e in self.ENGS}

    class _Phase:
        def __init__(self, P):
            self.P = P

        def __enter__(self):
            P = self.P
            self.saved = P.stack
            self.es = ExitStack()
            self.es.__enter__()
            P.stack = self.es
            P.barrier()
            return P

        def __exit__(self, *a):
            P = self.P
            P.barrier()
            P.flush()
            P.stack = self.saved
            self.es.__exit__(None, None, None)
            return False

    def phase(self):
        return Prog._Phase(self)


WNAMES = (("w1_gate", D, D), ("w1_up", D, D), ("w1_down", D, D), ("w_in", D, IN_COLS), ("pool_w", 4 * 384, 384),
          ("w_gate_up", 512, 4 * D), ("w_branch_a", MIXW, D), ("w_branch_b", 512, D), ("w_branch_c", MIXW, D),
          ("w_branch_d", MIXW, D), ("w_out", D, D), ("w2_gate", D, D), ("w2_up", D, D), ("w2_down", D, D))
WGROUPS = ((0, 3), (3, 4), (4, 11), (11, 14))
GTOT = [sum(k * n for _, k, n in WNAMES[a:b]) for a, b in WGROUPS]
G8 = [g // NCORES for g in GTOT]
G8P = [g // 128 for g in G8]
EH = 12 * 128 * 8192
HALO = 20
BPATS = ((1, 512, (1024, 1536)), (4, 256, (256,)), (16, 64, (64,)))


def b_units():
    units = []
    mi = 0
    for g, (d, QB, q0s) in enumerate(BPATS):
        for q0 in q0s:
            k0s = list(range(((q0 - 64) // 128) * 128, q0 + QB + 64, 128))
            units.append((g, d, QB, q0, k0s, mi))
            mi += len(k0s)
    return units, mi


class Builder:
    def __init__(self, L):
        self.S = S = SL * NCORES
        self.L = L
        nc = self.nc = bass.Bass("TRN2", target_bir_lowering=False)
        di = lambda name, shape, dt=F32: nc.dram_tensor(name, list(shape), dt, kind="ExternalInput").ap()
        self.xT = di("xT", [D, SL])
        self.wsh = [[di(f"wsh_{l}_{g}", [128, G8P[g]]) for g in range(4)] for l in range(L)]
        self.NV = 32 * 3 + 4 + 2 + 36 + 12 + 128
        self.vecs = [di(f"vecs_{l}", [128, self.NV]) for l in range(L)]
        self.lnf = di("lnf", [128, KC])
        self.cosF = di("cosF", [128, SL])
        self.sinF = di("sinF", [128, SL])
        self.rmat = di("rmat", [128, 128], BF16)
        self.ident = di("ident", [128, 128], BF16)
        self.units, self.nmask = b_units()
        self.masks = di("masks", [128, self.nmask, 512], BF16)
        self.edge = di("edge", [128, 4, 16])
        self.oh = di("oh", [128, 24])
        self.outT = nc.dram_tensor("outT", [D, SL], F32, kind="ExternalOutput").ap()
        ds_ = lambda name, shape, dt: nc.dram_tensor(name, list(shape), dt).ap()
        wpad_shared = [ds_(f"wpad_{g}", [GTOT[g]], BF16) for g in range(4)]
        wfull_shared = [ds_(f"wfull_{g}", [GTOT[g]], BF16) for g in range(4)]
        self.wpad = [wpad_shared for l in range(L)]
        self.wfull = [wfull_shared for l in range(L)]
        self.epad = ds_("epad", [4 * EH], BF16)
        self.efull = ds_("efull", [4 * EH], BF16)
        self.hpad = ds_("hpad", [128, 8 * 12 * HALO], F32)
        self.hfull = ds_("hfull", [128, 8 * 12 * HALO], F32)
        self.kwin = ds_("kwin", [12, 128, LW], BF16)
        self.vwin = ds_("vwin", [LW, MIXW], BF16)
        self.xres = ds_("xres", [D, SL], F32)
        self.qaT = ds_("qaT", [12, 128, SL], BF16)
        self.qbT = ds_("qbT", [12, 128, SL], BF16)
        self.cbT = ds_("cbT", [12, 128, SL], F32)
        self.ccT = ds_("ccT", [12, 128, SL], F32)
        self.cuT = ds_("cuT", [12, 128, SL], F32)
        self.puT = ds_("puT", [12, 128, SL], F32)
        self.glT = ds_("glT", [4, 128, SL], BF16)
        self.yT = ds_("yT", [40, 128, SL], BF16)
        self.mgT = ds_("mgT", [KC, 128, SL], BF16)
        self.ka_pad = self.epad[0:EH].rearrange("(c p t) -> c p t", p=128, t=self.S)
        self.va_pad = self.epad[EH:2 * EH].rearrange("(t e) -> t e", e=MIXW)
        self.kb_pad = self.epad[2 * EH:3 * EH].rearrange("(c p t) -> c p t", p=128, t=self.S)
        self.vb_pad = self.epad[3 * EH:4 * EH].rearrange("(t e) -> t e", e=MIXW)
        self.ka_full = self.efull[0:EH].rearrange("(c p t) -> c p t", p=128, t=self.S)
        self.va_full = self.efull[EH:2 * EH].rearrange("(t e) -> t e", e=MIXW)
        self.kb_full = self.efull[2 * EH:3 * EH].rearrange("(c p t) -> c p t", p=128, t=self.S)
        self.vb_full = self.efull[3 * EH:4 * EH].rearrange("(t e) -> t e", e=MIXW)
        self.woff = {}
        for g, (a, b) in enumerate(WGROUPS):
            off = 0
            for nm, k, n in WNAMES[a:b]:
                self.woff[nm] = (g, off, k, n)
                off += k * n

    def wap(self, l, nm):
        g, off, k, n = self.woff[nm]
        return self.wfull[l][g][off:off + k * n].rearrange("(k n) -> k n", n=n)

    V_LN1, V_LNM, V_LN2 = 0, 32, 64
    V_LAM = 96
    V_SUB = 100
    V_CONV = 102
    V_PSC = 138
    V_BG = 150

    def consts(self, P):
        c = {}
        c["ones"] = P.sbuf("ones", [128, 128], BF16)
        c["ident"] = P.sbuf("ident", [128, 128], BF16)
        c["rmat"] = P.sbuf("rmat", [128, 128], BF16)
        c["vec"] = P.sbuf("vec", [128, self.NV], F32)
        c["oh"] = P.sbuf("oh", [128, 24], F32)
        P.op("pool", lambda e: e.memset(c["ones"][:], 1.0), writes=["ones"])
        P.dma("sp", c["ident"][:], self.ident, writes=["ident"])
        P.dma("sp", c["rmat"][:], self.rmat, writes=["rmat"])
        P.dma("sp", c["oh"][:], self.oh, writes=["oh"])
        return c

    def load_vecs(self, P, c, l):
        P.dma("sp", c["vec"][:], self.vecs[l], writes=["vec"])

    def gather_weights(self, P, c, l):
        FT = 4096
        wt = [P.sbuf(f"wt{i}", [128, FT], BF16) for i in range(2)]
        wo = [P.sbuf(f"wo{i}", [128, FT], BF16) for i in range(6)]
        oh = c["oh"]
        oi = 0
        ti = 0
        for g in range(4):
            pad = self.wpad[l][g].rearrange("(r p f) -> r p f", r=NCORES, p=128)
            for f0 in range(0, G8P[g], FT):
                fw = min(FT, G8P[g] - f0)
                b = ti % 2
                ti += 1
                P.dma("pool", wt[b][:, 0:fw], self.wsh[l][g][:, f0:f0 + fw], writes=[("wt", b)])
                for r in range(NCORES):
                    o = oi % 6
                    oi += 1
                    eng = "dve" if r % 2 == 0 else "pool"
                    P.op(eng, lambda e, o=o, b=b, r=r, fw=fw: e.tensor_scalar(out=wo[o][:, 0:fw], in0=wt[b][:, 0:fw], scalar1=oh[:, r:r + 1], scalar2=None, op0=ALU.mult),
                         reads=[("wt", b), "oh"], writes=[("wo", o)])
                    P.dma("sp", pad[r, :, f0:f0 + fw], wo[o][:, 0:fw], reads=[("wo", o)], writes=["wpad"])
        P.barrier()
        for g in range(4):
            pin = self.wpad[l][g].rearrange("(a b) -> a b", b=2048)
            pout = self.wfull[l][g].rearrange("(a b) -> a b", b=2048)
            rows = GTOT[g] // 2048
            step = rows // 2
            for i in range(2):
                P.cc(pin[i * step:(i + 1) * step, :], pout[i * step:(i + 1) * step, :], writes=["wfull"])

    def norm(self, P, c, src, t0, gain_ap_fn, h, bufs, ps_aux, out_f32=None):
        stage, sq, rstd = bufs["stage"], bufs["sq"], bufs["rstd"]
        for k in range(KC):
            s = k % 2
            P.dma("sp", stage[s][:], src[k * 128:(k + 1) * 128, t0:t0 + T], writes=[("stage", s)])
            P.op("act", lambda e, s=s: e.activation(out=sq[s][:], in_=stage[s][:], func=AF.Square),
                 reads=[("stage", s)], writes=[("sq", s)])
            for n in range(2):
                P.op("pe", lambda e, s=s, n=n, k=k: e.matmul(ps_aux[n][:], lhsT=c["ones"][:], rhs=sq[s][:, n * 512:(n + 1) * 512],
                                                        start=(k == 0), stop=(k == KC - 1)),
                     reads=[("sq", s), "ones"], writes=[("psx", n)], sig=(k == KC - 1 or n == 1))
        for n in range(2):
            sl = slice(n * 512, (n + 1) * 512)
            P.op("dve", lambda e, n=n, sl=sl: e.tensor_scalar(out=rstd[:, sl], in0=ps_aux[n][:], scalar1=1.0 / D, scalar2=EPS,
                                                         op0=ALU.mult, op1=ALU.add), reads=[("psx", n)], writes=[("rstd", n)])
            P.op("act", lambda e, sl=sl: e.activation(out=rstd[:, sl], in_=rstd[:, sl], func=AF.Sqrt),
                 reads=[("rstd", n)], writes=[("rstd", n)])
            P.op("dve", lambda e, sl=sl: e.reciprocal(out=rstd[:, sl], in_=rstd[:, sl]),
                 reads=[("rstd", n)], writes=[("rstd", n)])
        for k in range(KC):
            s = k % 2
            P.dma("sp", stage[s][:], src[k * 128:(k + 1) * 128, t0:t0 + T], writes=[("stage", s)])
            if out_f32 is None:
                P.op("dve", lambda e, s=s, k=k: e.scalar_tensor_tensor(out=h[:, k, :], in0=stage[s][:], scalar=gain_ap_fn(k), in1=rstd[:],
                                                                  op0=ALU.mult, op1=ALU.mult),
                     reads=[("stage", s), "vec", ("rstd", 0), ("rstd", 1)], writes=[("h", k)])
            else:
                o = bufs["ost"][k % 2]
                P.op("dve", lambda e, s=s, k=k, o=o: e.scalar_tensor_tensor(out=o[:], in0=stage[s][:], scalar=gain_ap_fn(k), in1=rstd[:],
                                                                       op0=ALU.mult, op1=ALU.mult),
                     reads=[("stage", s), "vec", ("rstd", 0), ("rstd", 1)], writes=[("ost", k % 2)])
                P.dma("sp", out_f32[k * 128:(k + 1) * 128, t0:t0 + T], o[:], reads=[("ost", k % 2)], writes=[("outT", k)])

    def gemm(self, P, wpan, ps, st, w_ap, col0, ncols, kc_n, rhs, rhs_keys, evac):
        wv = w_ap.rearrange("(kc p) n -> p kc n", p=128)
        npan = (kc_n + 15) // 16
        g0 = 0
        while g0 < ncols:
            gw = min(256, ncols - g0)
            pans = []
            for pg in range(npan):
                b = st["pi"] % len(wpan)
                st["pi"] += 1
                k0, k1 = pg * 16, min(kc_n, pg * 16 + 16)
                P.dma("pool", wpan[b][:, 0:k1 - k0, 0:gw], wv[:, k0:k1, col0 + g0:col0 + g0 + gw], writes=[("wpan", b)])
                pans.append(b)
            for m in range(gw // 128):
                bs = (st["si"] % 2) * 2
                st["si"] += 1
                for k in range(kc_n):
                    b = pans[k // 16]
                    for n in range(2):
                        P.op("pe", lambda e, b=b, k=k, m=m, n=n, bs=bs: e.matmul(
                            ps[bs + n][:], lhsT=wpan[b][:, k % 16, m * 128:(m + 1) * 128], rhs=rhs(k, n),
                            start=(k == 0), stop=(k == kc_n - 1)),
                             reads=[("wpan", b)] + rhs_keys(k), writes=[("ps", bs + n)], sig=(k == kc_n - 1))
                for n in range(2):
                    evac((col0 + g0) // 128 + m, n, ps[bs + n], ("ps", bs + n))
            g0 += gw

    def ffn(self, P, c, l, t0, wg, wu, wd, vcol):
        h = P.sbuf("h", [128, KC, T], BF16)
        a = P.sbuf("a", [128, KC, T], BF16)
        wpan = [P.sbuf(f"wpan{i}", [128, 16, 256], BF16) for i in range(4)]
        bufs = {"stage": [P.sbuf(f"stage{i}", [128, T], F32) for i in range(2)],
                "sq": [P.sbuf(f"sq{i}", [128, T], BF16) for i in range(2)],
                "rstd": P.sbuf("rstd", [128, T], F32)}
        sg = [P.sbuf(f"sg{i}", [128, 512], F32) for i in range(4)]
        xo = [P.sbuf(f"xo{i}", [128, T], F32) for i in range(2)]
        ps = [P.psum(f"ps{i}", [128, 512]) for i in range(8)]
        st = {"pi": 0, "si": 0}
        vec = c["vec"]
        self.norm(P, c, self.xres, t0, lambda k: vec[:, vcol + k:vcol + k + 1], h, bufs, ps[4:6])
        rhs_h = lambda k, n: h[:, k, n * 512:(n + 1) * 512]
        keys_h = lambda k: [("h", k)]

        def ev_gate(mg, n, pst, pkey):
            i = (mg % 2) * 2 + n
            P.op("act", lambda e, i=i, pst=pst: e.activation(out=sg[i][:], in_=pst[:], func=AF.Silu),
                 reads=[pkey], writes=[("sg", i)])

        def ev_up(mg, n, pst, pkey):
            i = (mg % 2) * 2 + n
            P.op("dve", lambda e, i=i, pst=pst, n=n, mg=mg: e.tensor_tensor(out=a[:, mg, n * 512:(n + 1) * 512], in0=pst[:], in1=sg[i][:], op=ALU.mult),
                 reads=[pkey, ("sg", i)], writes=[("a", mg)])
        for g0 in range(0, D, 256):
            self.gemm(P, wpan, ps, st, wg, g0, 256, KC, rhs_h, keys_h, ev_gate)
            self.gemm(P, wpan, ps, st, wu, g0, 256, KC, rhs_h, keys_h, ev_up)
        rhs_a = lambda k, n: a[:, k, n * 512:(n + 1) * 512]
        keys_a = lambda k: [("a", k)]

        def ev_down(mg, n, pst, pkey):
            s = mg % 2
            if n == 0:
                P.dma("sp", xo[s][:], self.xres[mg * 128:(mg + 1) * 128, t0:t0 + T], writes=[("xo", s)])
            P.op("dve", lambda e, s=s, n=n, pst=pst: e.scalar_tensor_tensor(out=xo[s][:, n * 512:(n + 1) * 512], in0=pst[:], scalar=0.5,
                                                                       in1=xo[s][:, n * 512:(n + 1) * 512], op0=ALU.mult, op1=ALU.add),
                 reads=[pkey, ("xo", s)], writes=[("xo", s)])
            if n == 1:
                P.dma("sp", self.xres[mg * 128:(mg + 1) * 128, t0:t0 + T], xo[s][:], reads=[("xo", s)], writes=[("xres", mg)])
        self.gemm(P, wpan, ps, st, wd, 0, D, KC, rhs_a, keys_a, ev_down)

    def proj_in(self, P, c, l):
        t0 = 0
        S = self.S
        oh = c["oh"]
        h = P.sbuf("h", [128, KC, T], BF16)
        wpan = [P.sbuf(f"wpan{i}", [128, 16, 256], BF16) for i in range(4)]
        bufs = {"stage": [P.sbuf(f"stage{i}", [128, T], F32) for i in range(2)],
                "sq": [P.sbuf(f"sq{i}", [128, T], BF16) for i in range(2)],
                "rstd": P.sbuf("rstd", [128, T], F32)}
        cosT = P.sbuf("cosT", [128, T], F32)
        sinT = P.sbuf("sinT", [128, T], F32)
        qbf = [P.sbuf(f"qbf{i}", [128, 512], BF16) for i in range(2)]
        t1 = [P.sbuf(f"t1_{i}", [128, 512], F32) for i in range(2)]
        t2 = [P.sbuf(f"t2_{i}", [128, 512], F32) for i in range(2)]
        obf = [P.sbuf(f"obf{i}", [128, T], BF16) for i in range(2)]
        of32 = [P.sbuf(f"of{i}", [128, T], F32) for i in range(2)]
        kpd = [P.sbuf(f"kpd{i}", [128, NCORES, T], BF16) for i in range(1)]
        vst = P.sbuf("vst", [128, 8, MIXW], BF16)
        vpd = [P.sbuf(f"vpd{i}", [128, MIXW], BF16) for i in range(4)]
        ps = [P.psum(f"ps{i}", [128, 512]) for i in range(6)]
        pst = [P.psum(f"pst{i}", [128, 1024], BF16) for i in range(2)]
        st = {"pi": 0, "si": 0, "ob": 0, "of": 0, "tp": 0, "kp": 0}
        vec = c["vec"]
        P.dma("sp", cosT[:], self.cosF[:, t0:t0 + T], writes=["cos"])
        P.dma("sp", sinT[:], self.sinF[:, t0:t0 + T], writes=["sin"])
        self.norm(P, c, self.xres, t0, lambda k: vec[:, self.V_LNM + k:self.V_LNM + k + 1], h, bufs, ps[4:6])
        rhs_h = lambda k, n: h[:, k, n * 512:(n + 1) * 512]
        keys_h = lambda k: [("h", k)]

        def mk_rope(dst, pad=None, win=None):
            def ev(mg, n, pm, pkey):
                i = n
                sl = slice(n * 512, (n + 1) * 512)
                if n == 0:
                    st["cur_ob"] = st["ob"] % 2
                    st["ob"] += 1
                ob = st["cur_ob"]
                P.op("act", lambda e, i=i, pm=pm: e.activation(out=qbf[i][:], in_=pm[:], func=AF.Copy),
                     reads=[pkey], writes=[("qbf", i)])
                P.op("pe", lambda e, i=i: e.matmul(ps[4 + i][:], lhsT=c["rmat"][:], rhs=qbf[i][:], start=True, stop=True),
                     reads=[("qbf", i), "rmat"], writes=[("psx", i)])
                P.op("dve", lambda e, i=i, sl=sl: e.tensor_tensor(out=t1[i][:], in0=ps[4 + i][:], in1=sinT[:, sl], op=ALU.mult),
                     reads=[("psx", i), "sin"], writes=[("t1", i)])
                P.op("dve", lambda e, i=i, sl=sl, pm=pm: e.tensor_tensor(out=t2[i][:], in0=pm[:], in1=cosT[:, sl], op=ALU.mult),
                     reads=[pkey, "cos"], writes=[("t2", i)])
                P.op("dve", lambda e, i=i, sl=sl, ob=ob: e.tensor_tensor(out=obf[ob][:, sl], in0=t1[i][:], in1=t2[i][:], op=ALU.add),
                     reads=[("t1", i), ("t2", i)], writes=[("obf", ob)])
                if n == 1:
                    ch = mg - dst[1]
                    if dst[0] is not None:
                        P.dma("sp", dst[0][ch, :, :], obf[ob][:], reads=[("obf", ob)], writes=[("dst", dst[2], ch)])
                    if win is not None:
                        P.dma("sp", win[ch, :, SL:2 * SL], obf[ob][:], reads=[("obf", ob)], writes=[("win", dst[2], ch)])
                    if pad is not None:
                        kp = 0
                        for r in range(NCORES):
                            eng = "pool" if r % 2 == 0 else "dve"
                            P.op(eng, lambda e, kp=kp, r=r, ob=ob: e.tensor_scalar(out=kpd[kp][:, r, :], in0=obf[ob][:], scalar1=oh[:, r:r + 1], scalar2=None, op0=ALU.mult),
                                 reads=[("obf", ob), "oh"], writes=[("kpd", kp)])
                        P.dma("sp", pad[ch].rearrange("p (r t) -> p r t", r=NCORES), kpd[kp][:], reads=[("kpd", kp)], writes=[("pad", dst[2], ch)])
            return ev

        def mk_f32(dst):
            def ev(mg, n, pm, pkey, dst=dst):
                sl = slice(n * 512, (n + 1) * 512)
                if n == 0:
                    st["cur_of"] = st["of"] % 2
                    st["of"] += 1
                ob = st["cur_of"]
                if n == 0:
                    P.op("act", lambda e, ob=ob, sl=sl, pm=pm: e.activation(out=of32[ob][:, sl], in_=pm[:], func=AF.Copy),
                         reads=[pkey], writes=[("of", ob)])
                else:
                    P.op("dve", lambda e, ob=ob, sl=sl, pm=pm: e.tensor_copy(out=of32[ob][:, sl], in_=pm[:]),
                         reads=[pkey], writes=[("of", ob)])
                    ch = mg - dst[1]
                    P.dma("sp", dst[0][ch, :, :], of32[ob][:], reads=[("of", ob)], writes=[("dst", dst[2], ch)])
            return ev

        def mk_bf(dst):
            def ev(mg, n, pm, pkey, dst=dst):
                sl = slice(n * 512, (n + 1) * 512)
                if n == 0:
                    st["cur_ob"] = st["ob"] % 2
                    st["ob"] += 1
                ob = st["cur_ob"]
                P.op("act", lambda e, ob=ob, sl=sl, pm=pm: e.activation(out=obf[ob][:, sl], in_=pm[:], func=AF.Copy),
                     reads=[pkey], writes=[("obf", ob)])
                if n == 1:
                    ch = mg - dst[1]
                    P.dma("sp", dst[0][ch, :, :], obf[ob][:], reads=[("obf", ob)], writes=[("dst", dst[2], ch)])
            return ev

        def mk_v(c0, name, pad, win):
            def ev(mg, n, pm, pkey):
                i = n
                ch = mg - c0
                P.op("act", lambda e, i=i, pm=pm: e.activation(out=qbf[i][:], in_=pm[:], func=AF.Copy),
                     reads=[pkey], writes=[("qbf", i)])
                tp = st["tp"] % 2
                st["tp"] += 1
                for j in range(4):
                    P.op("pe", lambda e, i=i, j=j, tp=tp: e.transpose(pst[tp][:, j * 128:(j + 1) * 128], qbf[i][:, j * 128:(j + 1) * 128], c["ident"][:]),
                         reads=[("qbf", i), "ident"], writes=[("pst", tp)], sig=(j == 3))
                P.op("dve", lambda e, tp=tp, n=n, ch=ch: e.tensor_copy(
                    out=vst[:, n * 4:(n + 1) * 4, ch * 128:(ch + 1) * 128],
                    in_=pst[tp][:, 0:512].rearrange("p (j e) -> p j e", j=4)),
                     reads=[("pst", tp)], writes=["vst"])
                if n == 1 and ch == 11:
                    if win is not None:
                        P.dma("sp", win[SL:2 * SL, :].rearrange("(j p) e -> p j e", p=128), vst[:], reads=["vst"], writes=[("vwin", name)])
                    vi = 0
                    for j in range(8):
                        for r in range(NCORES):
                            vp = vi % 4
                            vi += 1
                            eng = "pool" if vi % 2 == 0 else "dve"
                            P.op(eng, lambda e, vp=vp, r=r, j=j: e.tensor_scalar(out=vpd[vp][:], in0=vst[:, j, :], scalar1=oh[:, r:r + 1], scalar2=None, op0=ALU.mult),
                                 reads=["vst", "oh"], writes=[("vpd", vp)])
                            P.dma("sp", pad[r * SL + j * 128:r * SL + (j + 1) * 128, :], vpd[vp][:], reads=[("vpd", vp)], writes=[("vpad", name, r, j)])
            return ev

        W = self.wap(l, "w_in")
        fams = [(0, 12, mk_rope((self.qaT, 0, "qa"))),
                (12, 12, mk_rope((None, 12, "ka"), pad=self.ka_pad)),
                (24, 12, mk_v(24, "va", self.va_pad, None)),
                (36, 12, mk_rope((self.qbT, 36, "qb"))),
                (48, 12, mk_rope((None, 48, "kb"), pad=self.kb_pad, win=self.kwin)),
                (60, 12, mk_v(60, "vb", self.vb_pad, self.vwin)),
                (72, 12, mk_f32((self.cbT, 72, "cb"))), (84, 12, mk_f32((self.ccT, 84, "cc"))),
                (96, 12, mk_f32((self.cuT, 96, "cu"))), (108, 12, mk_f32((self.puT, 108, "pu"))),
                (120, 4, mk_bf((self.glT, 120, "gl")))]
        for (c0, nch, ev) in fams:
            self.gemm(P, wpan, ps, st, W, c0 * 128, nch * 128, KC, rhs_h, keys_h, ev)

    def exchange(self, P, c, l):
        oh = c["oh"]
        h0 = P.sbuf("h0", [128, 12, HALO], F32)
        hs = P.sbuf("hs", [128, NCORES, 12 * HALO], F32)
        P.dma("sp", h0[:, :, 0:1], self.ccT[:, :, 0:1].rearrange("c p t -> p c t"), writes=["h0"], slow=True)
        P.dma("sp", h0[:, :, 1:2], self.ccT[:, :, SL - 1:SL].rearrange("c p t -> p c t"), writes=["h0"], slow=True)
        P.dma("sp", h0[:, :, 2:3], self.cuT[:, :, 0:1].rearrange("c p t -> p c t"), writes=["h0"], slow=True)
        P.dma("sp", h0[:, :, 3:4], self.cuT[:, :, SL - 1:SL].rearrange("c p t -> p c t"), writes=["h0"], slow=True)
        P.dma("sp", h0[:, :, 4:12], self.puT[:, :, 0:8].rearrange("c p t -> p c t"), writes=["h0"], slow=True)
        P.dma("sp", h0[:, :, 12:20], self.puT[:, :, SL - 8:SL].rearrange("c p t -> p c t"), writes=["h0"], slow=True)
        for r in range(NCORES):
            P.op("dve", lambda e, r=r: e.tensor_scalar(out=hs[:, r, :], in0=h0[:].rearrange("p c t -> p (c t)"), scalar1=oh[:, r:r + 1], scalar2=None, op0=ALU.mult),
                 reads=["h0", "oh"], writes=["hs"])
        P.dma("sp", self.hpad, hs[:].rearrange("p r f -> p (r f)"), reads=["hs"], writes=["hpad"])
        P.barrier()
        ein = self.epad.rearrange("(a b) -> a b", b=2048)
        eout = self.efull.rearrange("(a b) -> a b", b=2048)
        rows = 4 * EH // 2048
        step = rows // 4
        for i in range(4):
            P.cc(ein[i * step:(i + 1) * step, :], eout[i * step:(i + 1) * step, :], writes=["efull"])
        P.cc(self.hpad, self.hfull, writes=["hfull"])

    def attn_a(self, P, c, l):
        S = self.S
        NKC = S // 128
        lam_init = 0.8 - 0.6 * math.exp(-0.3 * l)
        scale = HD ** -0.5
        vec = c["vec"]
        kT = [P.sbuf(f"kT{i}", [128, 2, S], BF16) for i in range(2)]
        vv = [P.sbuf(f"vv{i}", [128, NKC, 256], BF16) for i in range(2)]
        qt = [P.sbuf(f"qt{i}", [128, 2, 512], BF16) for i in range(2)]
        pT = [P.sbuf(f"pT{i}", [128, 512], BF16) for i in range(4)]
        rden = P.sbuf("rden", [128, 512], F32)
        oc = [[P.sbuf(f"oc{cc}{j}", [128, 512], F32) for j in range(2)] for cc in range(2)]
        osq = [P.sbuf(f"osq{j}", [128, 512], BF16) for j in range(2)]
        rs = P.sbuf("rs", [128, 512], F32)
        yst = [P.sbuf(f"yst{i}", [128, 2, 512], BF16) for i in range(2)]
        lam = P.sbuf("lam", [128, 4], F32)
        onesf = P.sbuf("onesf", [128, 128], F32)
        ps = [P.psum(f"ps{i}", [128, 512]) for i in range(8)]
        P.op("pool", lambda e: e.memset(onesf[:], 1.0), writes=["onesf"])
        P.op("dve", lambda e: e.tensor_tensor(out=lam[:, 0:1], in0=vec[:, self.V_LAM:self.V_LAM + 1], in1=vec[:, self.V_LAM + 1:self.V_LAM + 2], op=ALU.mult),
             reads=["vec"], writes=["lam"])
        P.op("dve", lambda e: e.tensor_tensor(out=lam[:, 1:2], in0=vec[:, self.V_LAM + 2:self.V_LAM + 3], in1=vec[:, self.V_LAM + 3:self.V_LAM + 4], op=ALU.mult),
             reads=["vec", "lam"], writes=["lam"])
        P.op("pe", lambda e: e.matmul(ps[0][:, 0:2], lhsT=onesf[:], rhs=lam[:, 0:2], start=True, stop=True),
             reads=["lam", "onesf"], writes=[("ps", 0)])
        P.op("act", lambda e: e.activation(out=lam[:, 0:2], in_=ps[0][:, 0:2], func=AF.Exp), reads=[("ps", 0)], writes=["lam"])
        P.op("dve", lambda e: e.tensor_tensor(out=lam[:, 2:3], in0=lam[:, 0:1], in1=lam[:, 1:2], op=ALU.subtract), reads=["lam"], writes=["lam"])
        P.op("dve", lambda e: e.tensor_scalar(out=lam[:, 2:3], in0=lam[:, 2:3], scalar1=lam_init, scalar2=None, op0=ALU.add), reads=["lam"], writes=["lam"])
        P.op("dve", lambda e: e.tensor_scalar(out=lam[:, 3:4], in0=lam[:, 2:3], scalar1=-1.0, scalar2=None, op0=ALU.mult), reads=["lam"], writes=["lam"])
        pi = 0
        yi = 0
        qi = 0
        for hh in range(6):
            hb = hh % 2
            for cc in range(2):
                P.dma("sp", kT[hb][:, cc, :], self.ka_full[2 * hh + cc], writes=[("kT", hb)])
            vsrc = self.va_full.rearrange("(kc p) e -> p kc e", p=128)
            for k0 in range(0, NKC, 16):
                P.dma("sp", vv[hb][:, k0:k0 + 16, :], vsrc[:, k0:k0 + 16, hh * 256:(hh + 1) * 256], writes=[("vv", hb)])
            for qb in range(SL // 512):
                qs = qi % 2
                qi += 1
                for cc in range(2):
                    P.dma("sp", qt[qs][:, cc, :], self.qaT[2 * hh + cc, :, qb * 512:(qb + 1) * 512], writes=[("qt", qs)])
                for cc in range(2):
                    ob = 2 + 3 * cc
                    for kc in range(NKC):
                        sb = kc % 2
                        p = pi % 4
                        pi += 1
                        P.op("pe", lambda e, hb=hb, cc=cc, kc=kc, sb=sb, qs=qs: e.matmul(
                            ps[sb][:], lhsT=kT[hb][:, cc, kc * 128:(kc + 1) * 128], rhs=qt[qs][:, cc, :], start=True, stop=True),
                             reads=[("kT", hb), ("qt", qs)], writes=[("ps", sb)])
                        P.op("act", lambda e, p=p, sb=sb: e.activation(out=pT[p][:], in_=ps[sb][:], func=AF.Exp, scale=scale),
                             reads=[("ps", sb)], writes=[("pT", p)])
                        first, last = (kc == 0), (kc == NKC - 1)
                        for j in range(2):
                            P.op("pe", lambda e, hb=hb, kc=kc, j=j, p=p, ob=ob, first=first, last=last: e.matmul(
                                ps[ob + j][:], lhsT=vv[hb][:, kc, j * 128:(j + 1) * 128], rhs=pT[p][:], start=first, stop=last),
                                 reads=[("vv", hb), ("pT", p)], writes=[("ps", ob + j)], sig=False)
                        P.op("pe", lambda e, p=p, ob=ob, first=first, last=last: e.matmul(
                            ps[ob + 2][:], lhsT=c["ones"][:], rhs=pT[p][:], start=first, stop=last),
                             reads=["ones", ("pT", p)], writes=[("ps", ob + 2)], sig=True)
                    P.op("dve", lambda e, ob=ob: e.reciprocal(out=rden[:], in_=ps[ob + 2][:]), reads=[("ps", ob + 2)], writes=["rden"])
                    for j in range(2):
                        P.op("dve", lambda e, cc=cc, j=j, ob=ob: e.tensor_tensor(out=oc[cc][j][:], in0=ps[ob + j][:], in1=rden[:], op=ALU.mult),
                             reads=[("ps", ob + j), "rden"], writes=[("oc", cc, j)])
                for j in range(2):
                    P.op("dve", lambda e, j=j: e.scalar_tensor_tensor(out=oc[0][j][:], in0=oc[1][j][:], scalar=lam[:, 3:4], in1=oc[0][j][:],
                                                                 op0=ALU.mult, op1=ALU.add),
                         reads=[("oc", 1, j), ("oc", 0, j), "lam"], writes=[("oc", 0, j)])
                    P.op("act", lambda e, j=j: e.activation(out=osq[j][:], in_=oc[0][j][:], func=AF.Square),
                         reads=[("oc", 0, j)], writes=[("osq", j)])
                    P.op("pe", lambda e, j=j: e.matmul(ps[0][:], lhsT=c["ones"][:], rhs=osq[j][:], start=(j == 0), stop=(j == 1)),
                         reads=[("osq", j), "ones"], writes=[("ps", 0)])
                P.op("dve", lambda e: e.tensor_scalar(out=rs[:], in0=ps[0][:], scalar1=1.0 / 256, scalar2=EPS, op0=ALU.mult, op1=ALU.add),
                     reads=[("ps", 0)], writes=["rs"])
                P.op("act", lambda e: e.activation(out=rs[:], in_=rs[:], func=AF.Sqrt), reads=["rs"], writes=["rs"])
                P.op("dve", lambda e: e.reciprocal(out=rs[:], in_=rs[:]), reads=["rs"], writes=["rs"])
                P.op("dve", lambda e: e.tensor_scalar(out=rs[:], in0=rs[:], scalar1=1.0 - lam_init, scalar2=None, op0=ALU.mult), reads=["rs"], writes=["rs"])
                ys = yi % 2
                yi += 1
                for j in range(2):
                    P.op("dve", lambda e, j=j, ys=ys: e.scalar_tensor_tensor(out=yst[ys][:, j, :], in0=oc[0][j][:], scalar=vec[:, self.V_SUB + j:self.V_SUB + j + 1],
                                                                        in1=rs[:], op0=ALU.mult, op1=ALU.mult),
                         reads=[("oc", 0, j), "rs", "vec"], writes=[("yst", ys)])
                    P.dma("sp", self.yT[2 * hh + j, :, qb * 512:(qb + 1) * 512], yst[ys][:, j, :], reads=[("yst", ys)], writes=[("yT", 2 * hh + j, qb)])

    def windows(self, P, c, l):
        S = self.S
        oh = c["oh"]
        kf = [P.sbuf(f"kf{i}", [128, S], BF16) for i in range(2)]
        vf = [P.sbuf(f"vf{i}", [128, S // 128, 128], BF16) for i in range(2)]
        wk = [P.sbuf(f"wk{i}", [128, 2, SL], BF16) for i in range(2)]
        zt = P.sbuf("zt", [128, SL], BF16)
        P.op("pool", lambda e: e.memset(zt[:], 0.0), writes=["zt"])
        for ch in range(12):
            P.dma("sp", self.kwin[ch, :, 3 * SL:4 * SL], zt[:], reads=["zt"], writes=[("kw3", ch)])
        for j in range(8):
            P.dma("sp", self.vwin[3 * SL + j * 128:3 * SL + (j + 1) * 128, :].rearrange("p (a e) -> p a e", e=128)[:, 0:8, :], zt[:].rearrange("p (a e) -> p a e", e=128),
                  reads=["zt"], writes=[("vw3", j, 0)])
            P.dma("sp", self.vwin[3 * SL + j * 128:3 * SL + (j + 1) * 128, 1024:1536], zt[:, 0:512], reads=["zt"], writes=[("vw3", j, 1)])
        for ch in range(12):
            b = ch % 2
            P.dma("sp", kf[b][:], self.kb_full[ch], writes=[("kf", b)])
            for side in range(2):
                for r in range(NCORES):
                    col = 8 + 8 * side + r
                    if r == 0:
                        P.op("dve", lambda e, b=b, side=side, r=r, col=col: e.tensor_scalar(out=wk[b][:, side, :], in0=kf[b][:, r * SL:(r + 1) * SL], scalar1=oh[:, col:col + 1], scalar2=None, op0=ALU.mult),
                             reads=[("kf", b), "oh"], writes=[("wk", b)])
                    else:
                        P.op("dve", lambda e, b=b, side=side, r=r, col=col: e.scalar_tensor_tensor(out=wk[b][:, side, :], in0=kf[b][:, r * SL:(r + 1) * SL], scalar=oh[:, col:col + 1], in1=wk[b][:, side, :], op0=ALU.mult, op1=ALU.add),
                             reads=[("kf", b), "oh", ("wk", b)], writes=[("wk", b)])
            P.dma("sp", self.kwin[ch, :, 0:SL], wk[b][:, 0, :], reads=[("wk", b)], writes=[("kw0", ch)])
            P.dma("sp", self.kwin[ch, :, 2 * SL:3 * SL], wk[b][:, 1, :], reads=[("wk", b)], writes=[("kw2", ch)])
        for ch in range(12):
            b = ch % 2
            src = self.vb_full[:, ch * 128:(ch + 1) * 128].rearrange("(j p) e -> p j e", p=128)
            for j0 in range(0, S // 128, 16):
                P.dma("sp", vf[b][:, j0:j0 + 16, :], src[:, j0:j0 + 16, :], writes=[("vf", b)])
            for side in range(2):
                for r in range(NCORES):
                    col = 8 + 8 * side + r
                    inn = vf[b][:, r * 8:(r + 1) * 8, :].rearrange("p j e -> p (j e)")
                    if r == 0:
                        P.op("dve", lambda e, b=b, side=side, inn=inn, col=col: e.tensor_scalar(out=wk[b][:, side, :], in0=inn, scalar1=oh[:, col:col + 1], scalar2=None, op0=ALU.mult),
                             reads=[("vf", b), "oh"], writes=[("wk", b)])
                    else:
                        P.op("dve", lambda e, b=b, side=side, inn=inn, col=col: e.scalar_tensor_tensor(out=wk[b][:, side, :], in0=inn, scalar=oh[:, col:col + 1], in1=wk[b][:, side, :], op0=ALU.mult, op1=ALU.add),
                             reads=[("vf", b), "oh", ("wk", b)], writes=[("wk", b)])
            for side in range(2):
                base = 0 if side == 0 else 2 * SL
                P.dma("sp", self.vwin[base:base + SL, ch * 128:(ch + 1) * 128].rearrange("(j p) e -> p j e", p=128),
                      wk[b][:, side, :].rearrange("p (j e) -> p j e", e=128), reads=[("wk", b)], writes=[("vw", side, ch)])

    def attn_b(self, P, c, l):
        scale = HD ** -0.5
        qs_ = [P.sbuf(f"qs{i}", [128, LW], BF16) for i in range(2)]
        ks_ = [P.sbuf(f"ks{i}", [128, LW], BF16) for i in range(2)]
        vs_ = [P.sbuf(f"vs{i}", [128, LW // 128, 128], BF16) for i in range(2)]
        msk = P.sbuf("msk", [128, self.nmask, 512], BF16)
        acc_o = P.sbuf("acc_o", [128, SL], F32)
        acc_d = P.sbuf("acc_d", [128, SL], F32)
        pT = [P.sbuf(f"pT{i}", [128, 512], BF16) for i in range(4)]
        yo = [P.sbuf(f"yo{i}", [128, SL], BF16) for i in range(2)]
        ps = [P.psum(f"ps{i}", [128, 512]) for i in range(8)]
        P.dma("sp", msk[:], self.masks, writes=["msk"])
        hi = 0
        pi = 0
        ui = 0
        for hd in range(4):
            for g, (d, QBg, q0s) in enumerate(BPATS):
                head = 4 * g + hd
                hb = hi % 2
                hi += 1
                Ls = LW // d
                nj = Ls // 128
                P.dma("sp", qs_[hb][:, SL:2 * SL], self.qbT[head], writes=[("qs", hb)])
                P.dma("sp", ks_[hb][:], self.kwin[head], writes=[("ks", hb)])
                for r in range(d):
                    src = self.vwin[:, head * 128:(head + 1) * 128].rearrange("(j p dd) e -> dd p j e", p=128, dd=d)[r]
                    for j0 in range(0, nj, 16):
                        j1 = min(nj, j0 + 16)
                        P.dma("sp", vs_[hb][:, r * nj + j0:r * nj + j1, :], src[:, j0:j1, :], writes=[("vs", hb)])
                for (ug, ud, QB, q0, k0s, mi) in self.units:
                    if ug != g:
                        continue
                    for r in range(d):
                        u = ui % 2
                        ui += 1
                        ob, db = 2 + 2 * u, 3 + 2 * u
                        qsl = slice(r + d * q0, r + d * (q0 + QB - 1) + 1, d)
                        osl = slice(r + d * q0 - SL, r + d * (q0 + QB - 1) + 1 - SL, d)
                        for ci, k0 in enumerate(k0s):
                            sb = ci % 2
                            p = pi % 4
                            pi += 1
                            ksl = slice(r + d * k0, r + d * (k0 + 127) + 1, d)
                            P.op("pe", lambda e, hb=hb, ksl=ksl, qsl=qsl, sb=sb, QB=QB: e.matmul(
                                ps[sb][:, 0:QB], lhsT=ks_[hb][:, ksl], rhs=qs_[hb][:, qsl], start=True, stop=False),
                                 reads=[("ks", hb), ("qs", hb)], writes=[("ps", sb)], sig=False)
                            P.op("pe", lambda e, mo=mi + ci, sb=sb, QB=QB: e.matmul(
                                ps[sb][:, 0:QB], lhsT=c["ident"][:], rhs=msk[:, mo, 0:QB], start=False, stop=True),
                                 reads=["ident", "msk"], writes=[("ps", sb)])
                            P.op("act", lambda e, p=p, sb=sb, QB=QB: e.activation(out=pT[p][:, 0:QB], in_=ps[sb][:, 0:QB], func=AF.Exp, scale=scale),
                                 reads=[("ps", sb)], writes=[("pT", p)])
                            first, last = (ci == 0), (ci == len(k0s) - 1)
                            vj = r * nj + k0 // 128
                            P.op("pe", lambda e, hb=hb, vj=vj, p=p, ob=ob, QB=QB, first=first, last=last: e.matmul(
                                ps[ob][:, 0:QB], lhsT=vs_[hb][:, vj, :], rhs=pT[p][:, 0:QB], start=first, stop=last),
                                 reads=[("vs", hb), ("pT", p)], writes=[("ps", ob)], sig=False)
                            P.op("pe", lambda e, p=p, db=db, QB=QB, first=first, last=last: e.matmul(
                                ps[db][:, 0:QB], lhsT=c["ones"][:], rhs=pT[p][:, 0:QB], start=first, stop=last),
                                 reads=["ones", ("pT", p)], writes=[("ps", db)])
                        if g == 0:
                            P.op("act", lambda e, ob=ob, osl=osl, QB=QB: e.activation(out=acc_o[:, osl], in_=ps[ob][:, 0:QB], func=AF.Copy),
                                 reads=[("ps", ob)], writes=["acc_o"])
                            P.op("dve", lambda e, db=db, osl=osl, QB=QB: e.tensor_copy(out=acc_d[:, osl], in_=ps[db][:, 0:QB]),
                                 reads=[("ps", db)], writes=["acc_d"])
                        else:
                            P.op("dve", lambda e, ob=ob, osl=osl, QB=QB: e.tensor_tensor(out=acc_o[:, osl], in0=ps[ob][:, 0:QB], in1=acc_o[:, osl], op=ALU.add),
                                 reads=[("ps", ob), "acc_o"], writes=["acc_o"])
                            P.op("dve", lambda e, db=db, osl=osl, QB=QB: e.tensor_tensor(out=acc_d[:, osl], in0=ps[db][:, 0:QB], in1=acc_d[:, osl], op=ALU.add),
                                 reads=[("ps", db), "acc_d"], writes=["acc_d"])
            y = hd % 2
            P.op("dve", lambda e: e.reciprocal(out=acc_d[:], in_=acc_d[:]), reads=["acc_d"], writes=["acc_d"])
            P.op("dve", lambda e, y=y: e.tensor_tensor(out=yo[y][:], in0=acc_o[:], in1=acc_d[:], op=ALU.mult),
                 reads=["acc_o", "acc_d"], writes=[("yo", y)])
            P.dma("sp", self.yT[12 + hd], yo[y][:], reads=[("yo", y)], writes=[("yT", 12 + hd)])

    def conv_pool(self, P, c, l):
        TW = SL
        vec = c["vec"]
        oh = c["oh"]
        hf = P.sbuf("hf", [128, NCORES, 12 * HALO], F32)
        hp = P.sbuf("hp", [128, 12, HALO], F32)
        hn = P.sbuf("hn", [128, 12, HALO], F32)
        zz = P.sbuf("zz", [128, TW + 2], F32)
        cu = P.sbuf("cu", [128, TW], F32)
        cb = P.sbuf("cb", [128, TW], F32)
        yy = P.sbuf("yy", [128, TW], F32)
        ycb = [P.sbuf(f"ycb{i}", [128, TW], BF16) for i in range(2)]
        uu = P.sbuf("uu", [128, TW + 16], F32)
        s1 = P.sbuf("s1", [128, TW + 16], F32)
        s2 = P.sbuf("s2", [128, TW + 16], F32)
        pd = P.sbuf("pd", [128, 3, TW], BF16)
        pw = P.sbuf("pw", [128, 12, 384], BF16)
        edge = P.sbuf("edge", [128, 4, 16], F32)
        ydb = [P.sbuf(f"ydb{i}", [128, TW], BF16) for i in range(2)]
        ps = [P.psum(f"ps{i}", [128, 512]) for i in range(8)]
        P.dma("sp", edge[:], self.edge, writes=["edge"])
        P.dma("sp", hf[:].rearrange("p r f -> p (r f)"), self.hfull, writes=["hf"])
        P.dma("pool", pw[:], self.wap(l, "pool_w").rearrange("(k p) n -> p k n", p=128), writes=["pw"])
        for side, dst in ((0, hp), (1, hn)):
            for r in range(NCORES):
                col = 8 + 8 * side + r
                dflat = dst[:].rearrange("p c t -> p (c t)")
                if r == 0:
                    P.op("dve", lambda e, r=r, col=col, dflat=dflat: e.tensor_scalar(out=dflat, in0=hf[:, r, :], scalar1=oh[:, col:col + 1], scalar2=None, op0=ALU.mult),
                         reads=["hf", "oh"], writes=["hsel"])
                else:
                    P.op("dve", lambda e, r=r, col=col, dflat=dflat: e.scalar_tensor_tensor(out=dflat, in0=hf[:, r, :], scalar=oh[:, col:col + 1], in1=dflat, op0=ALU.mult, op1=ALU.add),
                         reads=["hf", "oh", "hsel"], writes=["hsel"])
        yi = 0
        for i in range(12):
            P.dma("sp", zz[:, 1:TW + 1], self.ccT[i], writes=["zz"])
            P.dma("sp", cu[:], self.cuT[i], writes=["cu"])
            P.dma("sp", cb[:], self.cbT[i], writes=["cb"])
            P.op("dve", lambda e: e.tensor_tensor(out=zz[:, 1:TW + 1], in0=zz[:, 1:TW + 1], in1=cu[:], op=ALU.mult),
                 reads=["zz", "cu"], writes=["zz"])
            P.op("dve", lambda e, i=i: e.tensor_tensor(out=zz[:, 0:1], in0=hp[:, i, 1:2], in1=hp[:, i, 3:4], op=ALU.mult),
                 reads=["hsel", "zz"], writes=["zz"])
            P.op("dve", lambda e, i=i: e.tensor_tensor(out=zz[:, TW + 1:TW + 2], in0=hn[:, i, 0:1], in1=hn[:, i, 2:3], op=ALU.mult),
                 reads=["hsel", "zz"], writes=["zz"])
            wc = lambda k, i=i: vec[:, self.V_CONV + i * 3 + k:self.V_CONV + i * 3 + k + 1]
            P.op("dve", lambda e, wc=wc: e.tensor_scalar(out=yy[:], in0=zz[:, 0:TW], scalar1=wc(0), scalar2=None, op0=ALU.mult),
                 reads=["zz", "vec"], writes=["yy"])
            P.op("dve", lambda e, wc=wc: e.scalar_tensor_tensor(out=yy[:], in0=zz[:, 1:TW + 1], scalar=wc(1), in1=yy[:], op0=ALU.mult, op1=ALU.add),
                 reads=["zz", "vec", "yy"], writes=["yy"])
            P.op("dve", lambda e, wc=wc: e.scalar_tensor_tensor(out=yy[:], in0=zz[:, 2:TW + 2], scalar=wc(2), in1=yy[:], op0=ALU.mult, op1=ALU.add),
                 reads=["zz", "vec", "yy"], writes=["yy"])
            y = yi % 2
            yi += 1
            P.op("dve", lambda e, y=y: e.tensor_tensor(out=ycb[y][:], in0=yy[:], in1=cb[:], op=ALU.mult),
                 reads=["yy", "cb"], writes=[("ycb", y)])
            P.dma("sp", self.yT[16 + i], ycb[y][:], reads=[("ycb", y)], writes=[("yT", 16 + i)])
        for g in range(4):
            win = (2, 4, 8, 16)[g]
            for j in range(3):
                i = 3 * g + j
                P.dma("sp", uu[:, 8:8 + TW], self.puT[i], writes=["uu"])
                P.op("dve", lambda e, i=i: e.tensor_copy(out=uu[:, 0:8], in_=hp[:, i, 12:20]), reads=["hsel", "uu"], writes=["uu"])
                P.op("dve", lambda e, i=i: e.tensor_copy(out=uu[:, TW + 8:TW + 16], in_=hn[:, i, 4:12]), reads=["hsel", "uu"], writes=["uu"])
                W_ = TW + 16
                P.op("dve", lambda e, W_=W_: e.tensor_tensor(out=s1[:, 1:W_], in0=uu[:, 0:W_ - 1], in1=uu[:, 1:W_], op=ALU.add),
                     reads=["uu"], writes=["s1"])
                cur, oth = s1, s2
                lo_v, hi_v = 1, W_
                for (sh, wn) in ((1, 4), (2, 8), (4, 16)):
                    if win < wn:
                        break
                    a0, a1 = lo_v + sh, hi_v - sh
                    P.op("dve", lambda e, cur=cur, oth=oth, a0=a0, a1=a1, sh=sh: e.tensor_tensor(
                        out=oth[:, a0:a1], in0=cur[:, a0 - sh:a1 - sh], in1=cur[:, a0 + sh:a1 + sh], op=ALU.add),
                         reads=["s1", "s2"], writes=["s1", "s2"])
                    cur, oth = oth, cur
                    lo_v, hi_v = a0, a1
                assert lo_v <= 8 and hi_v >= TW + 8
                P.op("dve", lambda e, cur=cur, win=win: e.tensor_scalar(out=cur[:, 8:8 + TW], in0=cur[:, 8:8 + TW], scalar1=1.0 / win, scalar2=None, op0=ALU.mult),
                     reads=["s1", "s2"], writes=["s1", "s2"])
                P.op("dve", lambda e, cur=cur, g=g: e.tensor_tensor(out=cur[:, 8:16], in0=cur[:, 8:16], in1=edge[:, g, 0:8], op=ALU.mult),
                     reads=["s1", "s2", "edge"], writes=["s1", "s2"])
                P.op("dve", lambda e, cur=cur, g=g: e.tensor_tensor(out=cur[:, TW:TW + 8], in0=cur[:, TW:TW + 8], in1=edge[:, g, 8:16], op=ALU.mult),
                     reads=["s1", "s2", "edge"], writes=["s1", "s2"])
                P.op("dve", lambda e, cur=cur, j=j: e.tensor_tensor(out=pd[:, j, :], in0=cur[:, 8:8 + TW], in1=uu[:, 8:8 + TW], op=ALU.subtract),
                     reads=["s1", "s2", "uu"], writes=[("pd", j)])
            for dj in range(3):
                y = yi % 2
                yi += 1
                for nn in range(TW // 512):
                    b = (yi + nn) % 4
                    for kj in range(3):
                        P.op("pe", lambda e, g=g, kj=kj, dj=dj, nn=nn, b=b: e.matmul(
                            ps[b][:], lhsT=pw[:, 3 * g + kj, dj * 128:(dj + 1) * 128], rhs=pd[:, kj, nn * 512:(nn + 1) * 512],
                            start=(kj == 0), stop=(kj == 2)),
                             reads=["pw", ("pd", kj)], writes=[("ps", b)], sig=(kj == 2))
                    col = self.V_PSC + 3 * g + dj
                    P.op("dve", lambda e, b=b, y=y, nn=nn, col=col: e.tensor_scalar(out=ydb[y][:, nn * 512:(nn + 1) * 512], in0=ps[b][:],
                                                                                scalar1=vec[:, col:col + 1], scalar2=None, op0=ALU.mult),
                         reads=[("ps", b), "vec"], writes=[("ydb", y)])
                P.dma("sp", self.yT[28 + 3 * g + dj], ydb[y][:], reads=[("ydb", y)], writes=[("yT", 28 + 3 * g + dj)])

    def merge(self, P, c, l, t0):
        vec = c["vec"]
        yb = P.sbuf("yb", [128, 44, T], BF16)
        mgst = [P.sbuf(f"mgst{i}", [128, T], BF16) for i in range(2)]
        wpan = [P.sbuf(f"wpan{i}", [128, 16, 256], BF16) for i in range(4)]
        sg = [P.sbuf(f"sg{i}", [128, 512], F32) for i in range(4)]
        tmp = [P.sbuf(f"tmp{i}", [128, 512], F32) for i in range(2)]
        macc = P.sbuf("macc", [128, 2, 2, 512], F32)
        ps = [P.psum(f"ps{i}", [128, 512]) for i in range(8)]
        st = {"pi": 0, "si": 0, "ti": 0}
        for i in range(40):
            P.dma("sp", yb[:, i, :], self.yT[i, :, t0:t0 + T], writes=[("yb", i)])
        for i in range(4):
            P.dma("sp", yb[:, 40 + i, :], self.glT[i, :, t0:t0 + T], writes=[("yb", 40 + i)])
        branches = (("w_branch_a", 0, 12), ("w_branch_b", 12, 4), ("w_branch_c", 16, 12), ("w_branch_d", 28, 12))
        for g0 in range(0, D, 256):
            for bi, (wn, y0, nk) in enumerate(branches):
                def ev_gate(mgc, n, pst, pkey, bi=bi):
                    m = mgc % 2
                    i = m * 2 + n
                    col = self.V_BG + mgc
                    P.op("act", lambda e, i=i, pst=pst, col=col: e.activation(out=sg[i][:], in_=pst[:], func=AF.Sigmoid, bias=vec[:, col:col + 1]),
                         reads=[pkey, "vec"], writes=[("sg", i)])

                def ev_br(mgc, n, pst, pkey, bi=bi):
                    m = mgc % 2
                    i = m * 2 + n
                    if bi == 0:
                        P.op("dve", lambda e, i=i, m=m, n=n, pst=pst: e.tensor_tensor(out=macc[:, m, n, :], in0=pst[:], in1=sg[i][:], op=ALU.mult),
                             reads=[pkey, ("sg", i)], writes=[("macc", i)])
                    else:
                        tb = st["ti"] % 2
                        st["ti"] += 1
                        P.op("dve", lambda e, i=i, tb=tb, pst=pst: e.tensor_tensor(out=tmp[tb][:], in0=pst[:], in1=sg[i][:], op=ALU.mult),
                             reads=[pkey, ("sg", i)], writes=[("tmp", tb)])
                        if bi < 3:
                            P.op("dve", lambda e, m=m, n=n, tb=tb: e.tensor_tensor(out=macc[:, m, n, :], in0=macc[:, m, n, :], in1=tmp[tb][:], op=ALU.add),
                                 reads=[("tmp", tb), ("macc", i)], writes=[("macc", i)])
                        else:
                            P.op("dve", lambda e, m=m, n=n, tb=tb: e.tensor_tensor(out=mgst[m][:, n * 512:(n + 1) * 512], in0=macc[:, m, n, :], in1=tmp[tb][:], op=ALU.add),
                                 reads=[("tmp", tb), ("macc", i)], writes=[("mgst", m)])
                            if n == 1:
                                P.dma("sp", self.mgT[mgc, :, t0:t0 + T], mgst[m][:], reads=[("mgst", m)], writes=[("mgT", mgc)])
                self.gemm(P, wpan, ps, st, self.wap(l, "w_gate_up"), bi * D + g0, 256, 4,
                          lambda k, n: yb[:, 40 + k, n * 512:(n + 1) * 512], lambda k: [("yb", 40 + k)], ev_gate)
                self.gemm(P, wpan, ps, st, self.wap(l, wn), g0, 256, nk,
                          lambda k, n, y0=y0: yb[:, y0 + k, n * 512:(n + 1) * 512], lambda k, y0=y0: [("yb", y0 + k)], ev_br)

    def out_proj(self, P, c, l, t0):
        mg_ = P.sbuf("mg", [128, KC, T], BF16)
        wpan = [P.sbuf(f"wpan{i}", [128, 16, 256], BF16) for i in range(4)]
        xo = [P.sbuf(f"xo{i}", [128, T], F32) for i in range(2)]
        ps = [P.psum(f"ps{i}", [128, 512]) for i in range(8)]
        st = {"pi": 0, "si": 0}
        for k in range(KC):
            P.dma("sp", mg_[:, k, :], self.mgT[k, :, t0:t0 + T], writes=[("mg", k)])
        rhs_m = lambda k, n: mg_[:, k, n * 512:(n + 1) * 512]
        keys_m = lambda k: [("mg", k)]

        def ev_out(mgc, n, pst, pkey):
            s = mgc % 2
            if n == 0:
                P.dma("sp", xo[s][:], self.xres[mgc * 128:(mgc + 1) * 128, t0:t0 + T], writes=[("xo", s)])
            P.op("dve", lambda e, s=s, n=n, pst=pst: e.tensor_tensor(out=xo[s][:, n * 512:(n + 1) * 512], in0=pst[:], in1=xo[s][:, n * 512:(n + 1) * 512], op=ALU.add),
                 reads=[pkey, ("xo", s)], writes=[("xo", s)])
            if n == 1:
                P.dma("sp", self.xres[mgc * 128:(mgc + 1) * 128, t0:t0 + T], xo[s][:], reads=[("xo", s)], writes=[("xres", mgc)])
        self.gemm(P, wpan, ps, st, self.wap(l, "w_out"), 0, D, KC, rhs_m, keys_m, ev_out)

    def final_norm(self, P, c, t0):
        bufs = {"stage": [P.sbuf(f"stage{i}", [128, T], F32) for i in range(2)],
                "sq": [P.sbuf(f"sq{i}", [128, T], BF16) for i in range(2)],
                "rstd": P.sbuf("rstd", [128, T], F32),
                "ost": [P.sbuf(f"ost{i}", [128, T], F32) for i in range(2)]}
        lnf = P.sbuf("lnf", [128, KC], F32)
        ps = [P.psum(f"ps{i}", [128, 512]) for i in range(2)]
        P.dma("sp", lnf[:], self.lnf, writes=["vec"])
        self.norm(P, c, self.xres, t0, lambda k: lnf[:, k:k + 1], None, bufs, ps, out_f32=self.outT)

    def build(self):
        nc = self.nc
        with ExitStack() as gst:
            P = Prog(nc, gst)
            self.P = P
            c = self.consts(P)
            for k in range(0, D, 512):
                P.dma("sp", self.xres[k:k + 512, :], self.xT[k:k + 512, :], writes=[("xres0", k)])
            P.barrier()
            P.flush()
            for l in range(self.L):
                self.load_vecs(P, c, l)
                P.barrier()
                with P.phase():
                    self.gather_weights(P, c, l)
                with P.phase():
                    self.ffn(P, c, l, 0, self.wap(l, "w1_gate"), self.wap(l, "w1_up"), self.wap(l, "w1_down"), self.V_LN1)
                with P.phase():
                    self.proj_in(P, c, l)
                with P.phase():
                    self.exchange(P, c, l)
                with P.phase():
                    self.attn_a(P, c, l)
                with P.phase():
                    self.windows(P, c, l)
                with P.phase():
                    self.attn_b(P, c, l)
                with P.phase():
                    self.conv_pool(P, c, l)
                with P.phase():
                    self.merge(P, c, l, 0)
                with P.phase():
                    self.out_proj(P, c, l, 0)
                with P.phase():
                    self.ffn(P, c, l, 0, self.wap(l, "w2_gate"), self.wap(l, "w2_up"), self.wap(l, "w2_down"), self.V_LN2)
            with P.phase():
                self.final_norm(P, c, 0)
            P.barrier()
            P.flush()
        return nc


def host_constants(S):
    inv = (500000.0 ** (-np.arange(0, 32, 2, dtype=np.float32) / np.float32(32))).astype(np.float32)
    ang = np.arange(S, dtype=np.float32)[:, None] * inv[None, :]
    cos, sin = np.cos(ang).astype(np.float32), np.sin(ang).astype(np.float32)
    cosF = np.ones((128, S), np.float32)
    sinF = np.zeros((128, S), np.float32)
    cosF[0:16] = cos.T
    cosF[16:32] = cos.T
    sinF[0:16] = sin.T
    sinF[16:32] = sin.T
    rmat = np.zeros((128, 128), np.float32)
    for i in range(16):
        rmat[16 + i, i] = -1.0
        rmat[i, 16 + i] = 1.0
    bf = ml_dtypes.bfloat16
    return dict(cosF=cosF, sinF=sinF, rmat=rmat.astype(bf), ident=np.eye(128, dtype=np.float32).astype(bf))


def core_constants(cid, S):
    units, nmask = b_units()
    masks = np.full((128, nmask, 512), NEG, np.float32)
    kp = np.arange(128)[:, None]
    for (g, d, QB, q0, k0s, mi) in units:
        qf = np.arange(QB)[None, :]
        for ci, k0 in enumerate(k0s):
            band = np.abs(k0 + kp - q0 - qf) <= 64
            gl_lo = (cid - 1) * SL + d * (k0 + kp)
            valid = (gl_lo >= 0) & (gl_lo < S) & (d * (k0 + kp) < 3 * SL)
            masks[:, mi + ci, 0:QB] = np.where(band & valid, 0.0, NEG)
    edge = np.ones((128, 4, 16), np.float32)
    for g, w in enumerate((2, 4, 8, 16)):
        for j in range(8):
            for (t, col) in ((cid * SL + j, j), (cid * SL + SL - 8 + j, 8 + j)):
                cnt = min(t - w // 2 + w, S) - max(t - w // 2, 0)
                edge[:, g, col] = np.float32(w) / np.float32(cnt)
    oh = np.zeros((128, 24), np.float32)
    oh[:, cid] = 1.0
    if cid - 1 >= 0:
        oh[:, 8 + cid - 1] = 1.0
    if cid + 1 < NCORES:
        oh[:, 16 + cid + 1] = 1.0
    return dict(masks=masks.astype(ml_dtypes.bfloat16), edge=edge, oh=oh)


def col_layout(v):
    return np.ascontiguousarray(np.asarray(v, np.float32).reshape(-1, 128).T)


def make_in_maps(inputs, L):
    S = SL * NCORES
    x = np.asarray(inputs["x"], np.float32)[0]
    hc = host_constants(S)
    shared = {"rmat": hc["rmat"], "ident": hc["ident"], "lnf": col_layout(inputs["ln_final"])}
    for l in range(L):
        cols = [col_layout(inputs["ln_ffn1"][l]), col_layout(inputs["ln_mix"][l]), col_layout(inputs["ln_ffn2"][l])]
        cols += [np.asarray(inputs[k][l], np.float32).reshape(128, 1) for k in ("lambda_q1", "lambda_k1", "lambda_q2", "lambda_k2")]
        cols.append(col_layout(inputs["subln"][l]))
        cw = np.asarray(inputs["conv_w"][l], np.float32)
        cols.append(np.ascontiguousarray(cw.reshape(3, 12, 128).transpose(2, 1, 0).reshape(128, 36)))
        cols.append(col_layout(inputs["pool_scale"][l]))
        cols.append(col_layout(inputs["b_gate"][l]))
        shared[f"vecs_{l}"] = np.ascontiguousarray(np.concatenate(cols, axis=1))
    blobs = []
    for l in range(L):
        gl = []
        for g, (a, b) in enumerate(WGROUPS):
            blob = np.empty(GTOT[g], np.float32)
            off = 0
            for nm, k, n in WNAMES[a:b]:
                blob[off:off + k * n] = np.asarray(inputs[nm][l], np.float32).reshape(-1)
                off += k * n
            gl.append(blob)
        blobs.append(gl)
    maps = []
    for cid in range(NCORES):
        m = dict(shared)
        m["xT"] = np.ascontiguousarray(x[cid * SL:(cid + 1) * SL].T)
        m["cosF"] = np.ascontiguousarray(hc["cosF"][:, cid * SL:(cid + 1) * SL])
        m["sinF"] = np.ascontiguousarray(hc["sinF"][:, cid * SL:(cid + 1) * SL])
        for l in range(L):
            for g in range(4):
                m[f"wsh_{l}_{g}"] = blobs[l][g][cid * G8[g]:(cid + 1) * G8[g]].reshape(128, G8P[g])
        m.update(core_constants(cid, S))
        maps.append(m)
    return maps


_CACHE = {}


def kernel(**inputs):
    L = int(np.asarray(inputs["w_out"]).shape[0])
    if L not in _CACHE:
        _CACHE[L] = Builder(L).build()
    nc = _CACHE[L]
    in_maps = make_in_maps(inputs, L)
    res = run_bass_kernel_spmd(nc, in_maps, core_ids=list(range(NCORES)))
    outT = np.concatenate([np.asarray(res.results[c]["outT"]) for c in range(NCORES)], axis=1)
    return np.ascontiguousarray(outT.T)[None].astype(np.float32)
```
